# Optimizing a Trainium2 kernel written in Bass

```python
import jax
import jax.numpy as jnp
from jax import lax
import numpy as np

D_MODEL = 2048
BATCH = 8
SEQ = 2048
DEPTH = 2

GRID_W = 64
CTX_LEN = 256
NORM_EPS = 1e-6

GROUP_WIDTH = D_MODEL // 4

MLA_NOPE = 128
MLA_ROPE = 64
MLA_V = 128
MLA_HEADS = GROUP_WIDTH // MLA_V
MLA_Q_RANK = 384
MLA_KV_RANK = 128
ROPE_BASE = 10000.0
Q_BLOCK = 128

RWKV_HEAD = 64
RWKV_HEADS = GROUP_WIDTH // RWKV_HEAD
RWKV_WIDTH = RWKV_HEADS * RWKV_HEAD
DECAY_RANK = 96
ICL_RANK = 96
GATE_RANK = 256
RWKV_GN_EPS = 64e-5

POOL_WINDOWS = (2, 4, 8, 16)
POOL_WIDTH = GROUP_WIDTH
POOL_GROUP = POOL_WIDTH // len(POOL_WINDOWS)

CONV_WIDTH = GROUP_WIDTH
CONV_TAPS = 3

FF_HIDDEN = 4 * D_MODEL

MLA_COLS = MLA_Q_RANK + MLA_KV_RANK + MLA_ROPE
RWKV_COLS = 3 * RWKV_WIDTH + 2 * DECAY_RANK + 2 * ICL_RANK + GATE_RANK
POOL_COLS = POOL_WIDTH
CONV_COLS = 3 * CONV_WIDTH
IN_COLS = MLA_COLS + RWKV_COLS + POOL_COLS + CONV_COLS
MIX_WIDTH = MLA_HEADS * MLA_V + RWKV_WIDTH + POOL_WIDTH + CONV_WIDTH

kernel_name = "hybrid_mla_rwkv7_pool_conv_dit_trunk"


def split_cols(x, sizes):
    return jnp.split(x, np.cumsum(sizes)[:-1].tolist(), axis=-1)


def rmsnorm(x, g):
    xf = x.astype(jnp.float32)
    y = xf * lax.rsqrt(jnp.mean(jnp.square(xf), axis=-1, keepdims=True) + NORM_EPS)
    return (y * g.astype(jnp.float32)).astype(x.dtype)


def neighbours(u):
    zero = jnp.zeros_like(u[:, :1])
    prev = jnp.concatenate([zero, u[:, :-1]], axis=1)
    nxt = jnp.concatenate([u[:, 1:], zero], axis=1)
    return prev, nxt


def axial_rope_tables(n_tokens):
    rows = n_tokens // GRID_W
    row = jnp.repeat(jnp.arange(rows), GRID_W)
    col = jnp.tile(jnp.arange(GRID_W), rows)
    pos = jnp.stack([row, col], axis=-1).astype(jnp.float32)
    axis_dim = MLA_ROPE // 2
    inv_freq = ROPE_BASE ** (-jnp.arange(0, axis_dim, 2, dtype=jnp.float32) / axis_dim)
    ang = pos[:, :, None] * inv_freq
    return jnp.cos(ang), jnp.sin(ang)


def apply_rope(x, cos, sin):
    xf = x.astype(jnp.float32).reshape(x.shape[:-1] + (2, 2, MLA_ROPE // 4))
    x1, x2 = xf[..., 0, :], xf[..., 1, :]
    cs, sn = cos[None, :, None], sin[None, :, None]
    out = jnp.stack([x1 * cs - x2 * sn, x2 * cs + x1 * sn], axis=-2)
    return out.reshape(x.shape).astype(x.dtype)


def mla_keys(f_kv, f_kr, kv_norm_g, w_ukv, rope):
    b, n, _ = f_kv.shape
    kv = (rmsnorm(f_kv, kv_norm_g) @ w_ukv).reshape(b, n, MLA_HEADS, MLA_NOPE + MLA_V)
    k_nope, v = kv[..., :MLA_NOPE], kv[..., MLA_NOPE:]
    k_pe = f_kr[:, :, None, :]
    if rope is not None:
        k_pe = apply_rope(k_pe, *rope)
    return k_nope, k_pe[:, :, 0], v


def mla_queries(f_q, q_norm_g, w_uq, rope):
    b, n, _ = f_q.shape
    q = (rmsnorm(f_q, q_norm_g) @ w_uq).reshape(b, n, MLA_HEADS, MLA_NOPE + MLA_ROPE)
    q_nope, q_pe = q[..., :MLA_NOPE], q[..., MLA_NOPE:]
    if rope is not None:
        q_pe = apply_rope(q_pe, *rope)
    return q_nope, q_pe


def mla_attend(q_nope, q_pe, k_nope, k_pe, v):
    scale = (MLA_NOPE + MLA_ROPE) ** -0.5
    s = (jnp.einsum('bqhd,bkhd->bhqk', q_nope, k_nope)
         + jnp.einsum('bqhr,bkr->bhqk', q_pe, k_pe)).astype(jnp.float32) * scale
    p = jax.nn.softmax(s, axis=-1).astype(v.dtype)
    o = jnp.einsum('bhqk,bkhd->bqhd', p, v)
    return o.reshape(o.shape[0], o.shape[1], MLA_HEADS * MLA_V)


def mla_attend_blocked(q_nope, q_pe, k_nope, k_pe, v):
    b, n = q_nope.shape[:2]
    nb = n // Q_BLOCK
    qn = q_nope.reshape(b, nb, Q_BLOCK, MLA_HEADS, MLA_NOPE).swapaxes(0, 1)
    qp = q_pe.reshape(b, nb, Q_BLOCK, MLA_HEADS, MLA_ROPE).swapaxes(0, 1)
    o = lax.map(lambda qs: mla_attend(qs[0], qs[1], k_nope, k_pe, v), (qn, qp))
    return o.swapaxes(0, 1).reshape(b, n, MLA_HEADS * MLA_V)


def mla_mixer(fa_lat, fa_ctx, p, rope, need_ctx):
    q_l, kv_l, kr_l = split_cols(fa_lat, (MLA_Q_RANK, MLA_KV_RANK, MLA_ROPE))
    q_c, kv_c, kr_c = split_cols(fa_ctx, (MLA_Q_RANK, MLA_KV_RANK, MLA_ROPE))
    kn_c, kp_c, v_c = mla_keys(kv_c, kr_c, p['mla_kv_norm_g'], p['mla_w_ukv'], None)
    kn_l, kp_l, v_l = mla_keys(kv_l, kr_l, p['mla_kv_norm_g'], p['mla_w_ukv'], rope)
    qn_l, qp_l = mla_queries(q_l, p['mla_q_norm_g'], p['mla_w_uq'], rope)
    k_nope = jnp.concatenate([kn_c, kn_l], axis=1)
    k_pe = jnp.concatenate([kp_c, kp_l], axis=1)
    v = jnp.concatenate([v_c, v_l], axis=1)
    out_l = mla_attend_blocked(qn_l, qp_l, k_nope, k_pe, v)
    out_c = None
    if need_ctx:
        qn_c, qp_c = mla_queries(q_c, p['mla_q_norm_g'], p['mla_w_uq'], None)
        out_c = mla_attend(qn_c, qp_c, kn_c, kp_c, v_c)
    return out_l, out_c


def rwkv_prepare(f, p):
    f = f.astype(jnp.float32)
    b, n, _ = f.shape
    prev, nxt = neighbours(f)
    f = f + p['rwkv_mu'] * (0.5 * (prev + nxt) - f)
    r, k, v, wl, al, gl = split_cols(
        f, (RWKV_WIDTH, RWKV_WIDTH, RWKV_WIDTH, 2 * DECAY_RANK, 2 * ICL_RANK, GATE_RANK))
    wl = wl.reshape(b, n, 2, DECAY_RANK)
    al = al.reshape(b, n, 2, ICL_RANK)
    w = -jax.nn.softplus(-(p['rwkv_w0'] + jnp.einsum('bndr,drc->bndc', jnp.tanh(wl), p['rwkv_w2']))) - 0.5
    decay = jnp.exp(-jnp.exp(w))
    a = jax.nn.sigmoid(p['rwkv_a0'] + jnp.einsum('bndr,drc->bndc', al, p['rwkv_a2']))
    g = jax.nn.sigmoid(gl) @ p['rwkv_g2']

    def heads(t):
        return t.reshape(t.shape[:-1] + (RWKV_HEADS, RWKV_HEAD))

    kk = heads(k * p['rwkv_k_k'])
    kk = kk / jnp.maximum(jnp.sqrt(jnp.sum(kk * kk, axis=-1, keepdims=True)), 1e-12)
    k_dir = heads(k[:, :, None] * (1.0 + (a - 1.0) * p['rwkv_k_a']))
    r_h, v_h = heads(r), heads(v)
    b_dir = kk[:, :, None] * heads(a)
    bonus = jnp.einsum('bnhk,bndhk,hk->bnh', r_h, k_dir, p['rwkv_r_k'])[..., None] * v_h
    return {'r': r_h, 'w': heads(decay), 'k': k_dir, 'v': v_h, 'a': -kk, 'b': b_dir,
            'g': g, 'bonus': bonus}


def dir_inputs(feats, d):
    return (feats['r'], feats['w'][:, :, d], feats['k'][:, :, d], feats['v'], feats['a'], feats['b'][:, :, d])


def wkv_scan(state, r, w, k, v, a, b, reverse, with_out):
    xs = tuple(t.swapaxes(0, 1) for t in (r, w, k, v, a, b))

    def step(S, inp):
        r_t, w_t, k_t, v_t, a_t, b_t = inp
        sa = jnp.einsum('bhvk,bhk->bhv', S, a_t)
        S = S * w_t[:, :, None, :] + sa[..., None] * b_t[:, :, None, :] + v_t[..., None] * k_t[:, :, None, :]
        y = jnp.einsum('bhvk,bhk->bhv', S, r_t) if with_out else None
        return S, y

    state, ys = lax.scan(step, state, xs, reverse=reverse)
    return state, (ys.swapaxes(0, 1) if with_out else None)


def rwkv_output(y, feats, p, dtype):
    b, n = y.shape[:2]
    mean = jnp.mean(y, axis=-1, keepdims=True)
    var = jnp.mean(jnp.square(y - mean), axis=-1, keepdims=True)
    yn = ((y - mean) * lax.rsqrt(var + RWKV_GN_EPS)).reshape(b, n, RWKV_WIDTH)
    yn = yn * p['rwkv_ln_g'] + p['rwkv_ln_b']
    out = (yn + feats['bonus'].reshape(b, n, RWKV_WIDTH)) * feats['g']
    return out.astype(dtype)


def rwkv_mixer(fb_lat, fb_ctx, p, need_ctx):
    dtype = fb_lat.dtype
    fl = rwkv_prepare(fb_lat, p)
    fc = rwkv_prepare(fb_ctx, p)
    zero = jnp.zeros((fb_lat.shape[0], RWKV_HEADS, RWKV_HEAD, RWKV_HEAD), jnp.float32)
    s_cf, y_cf = wkv_scan(zero, *dir_inputs(fc, 0), reverse=False, with_out=need_ctx)
    s_cb, y_cb = wkv_scan(zero, *dir_inputs(fc, 1), reverse=True, with_out=need_ctx)
    _, y_lf = wkv_scan(s_cf, *dir_inputs(fl, 0), reverse=False, with_out=True)
    _, y_lb = wkv_scan(s_cb, *dir_inputs(fl, 1), reverse=True, with_out=True)
    out_l = rwkv_output(y_lf + y_lb, fl, p, dtype)
    out_c = rwkv_output(y_cf + y_cb, fc, p, dtype) if need_ctx else None
    return out_l, out_c


def pool_mixer(u, p):
    b, n, _ = u.shape
    uf = u.astype(jnp.float32)
    cs = jnp.concatenate([jnp.zeros_like(uf[:, :1]), jnp.cumsum(uf, axis=1)], axis=1)
    t = jnp.arange(n)
    groups = []
    for gi, win in enumerate(POOL_WINDOWS):
        sl = slice(gi * POOL_GROUP, (gi + 1) * POOL_GROUP)
        lo = jnp.clip(t - win // 2, 0, n)
        hi = jnp.clip(t + win // 2, 0, n)
        cnt = (hi - lo).astype(jnp.float32)[:, None]
        csg = cs[..., sl]
        mean = (jnp.take(csg, hi, axis=1) - jnp.take(csg, lo, axis=1)) / cnt
        groups.append(mean - uf[..., sl])
    z = jnp.stack(groups, axis=2)
    z = jnp.einsum('bngi,gio->bngo', z, p['pool_w'].astype(jnp.float32)).reshape(b, n, POOL_WIDTH)
    return (z * p['pool_scale']).astype(u.dtype)


def conv_mixer(fd, p):
    gb, gc, hx = split_cols(fd, (CONV_WIDTH, CONV_WIDTH, CONV_WIDTH))
    u = gc * hx
    prev, nxt = neighbours(u)
    w = p['conv_w']
    z = w[0] * prev + w[1] * u + w[2] * nxt
    return gb * z


def sq_relu_mlp(x, w1, w2):
    return jnp.square(jax.nn.relu(x @ w1)) @ w2


def trunk_layer(h, hc, c, c_ctx, p, rope, need_ctx):
    mod = jax.nn.silu(c) @ p['ada_w'] + p['ada_b']
    mod_c = jax.nn.silu(c_ctx) @ p['ada_w'] + p['ada_b']
    sh1, sc1, gt1, sh2, sc2, gt2 = [m[:, None, :] for m in jnp.split(mod, 6, axis=-1)]
    sh1c, sc1c, gt1c, sh2c, sc2c, gt2c = jnp.split(mod_c, 6, axis=-1)

    xn = rmsnorm(h, p['norm1_g']) * (1.0 + sc1) + sh1
    xc = rmsnorm(hc, p['norm1_g']) * (1.0 + sc1c) + sh1c
    col_sizes = (MLA_COLS, RWKV_COLS, POOL_COLS, CONV_COLS)
    fl = split_cols(xn @ p['w_in'], col_sizes)
    fc = split_cols(xc @ p['w_in'], col_sizes)

    att_l, att_c = mla_mixer(fl[0], fc[0], p, rope, need_ctx)
    rw_l, rw_c = rwkv_mixer(fl[1], fc[1], p, need_ctx)
    pool_l = pool_mixer(fl[2], p)
    conv_l = conv_mixer(fl[3], p)
    mixed = jnp.concatenate([att_l, rw_l, pool_l, conv_l], axis=-1) @ p['w_out']
    h = h + gt1 * mixed
    h = h + gt2 * sq_relu_mlp(rmsnorm(h, p['norm2_g']) * (1.0 + sc2) + sh2, p['mlp_w1'], p['mlp_w2'])

    if need_ctx:
        pool_c = pool_mixer(fc[2], p)
        conv_c = conv_mixer(fc[3], p)
        mixed_c = jnp.concatenate([att_c, rw_c, pool_c, conv_c], axis=-1) @ p['w_out']
        hc = hc + gt1c * mixed_c
        hc = hc + gt2c * sq_relu_mlp(rmsnorm(hc, p['norm2_g']) * (1.0 + sc2c) + sh2c, p['mlp_w1'], p['mlp_w2'])
    return h, hc


def setup_inputs(seed: int = 0) -> dict:
    key = jax.random.key(seed)
    ks = jax.random.split(key, 32)
    f32 = jnp.float32

    def nrm(k, shape, scale):
        return scale * jax.random.normal(k, shape, f32)

    L = DEPTH
    return {
        "x": nrm(ks[0], (BATCH, SEQ, D_MODEL), 1.0),
        "c": nrm(ks[1], (BATCH, D_MODEL), 1.0),
        "ctx": nrm(ks[2], (BATCH, CTX_LEN, D_MODEL), 1.0),
        "c_ctx": nrm(ks[3], (D_MODEL,), 1.0),
        "ada_w": nrm(ks[4], (L, D_MODEL, 6 * D_MODEL), 0.5 * D_MODEL ** -0.5),
        "ada_b": nrm(ks[5], (L, 6 * D_MODEL), 0.01),
        "norm1_g": 1.0 + nrm(ks[6], (L, D_MODEL), 0.02),
        "norm2_g": 1.0 + nrm(ks[7], (L, D_MODEL), 0.02),
        "w_in": nrm(ks[8], (L, D_MODEL, IN_COLS), D_MODEL ** -0.5),
        "mla_q_norm_g": 1.0 + nrm(ks[9], (L, MLA_Q_RANK), 0.02),
        "mla_w_uq": nrm(ks[10], (L, MLA_Q_RANK, MLA_HEADS * (MLA_NOPE + MLA_ROPE)), MLA_Q_RANK ** -0.5),
        "mla_kv_norm_g": 1.0 + nrm(ks[11], (L, MLA_KV_RANK), 0.02),
        "mla_w_ukv": nrm(ks[12], (L, MLA_KV_RANK, MLA_HEADS * (MLA_NOPE + MLA_V)), MLA_KV_RANK ** -0.5),
        "rwkv_mu": jax.random.uniform(ks[13], (L, RWKV_COLS), f32),
        "rwkv_w0": jax.random.uniform(ks[14], (L, 2, RWKV_WIDTH), f32, -6.0, -1.0),
        "rwkv_w2": nrm(ks[15], (L, 2, DECAY_RANK, RWKV_WIDTH), 0.5 * DECAY_RANK ** -0.5),
        "rwkv_a0": nrm(ks[16], (L, 2, RWKV_WIDTH), 0.1),
        "rwkv_a2": nrm(ks[17], (L, 2, ICL_RANK, RWKV_WIDTH), 0.5 * ICL_RANK ** -0.5),
        "rwkv_g2": nrm(ks[18], (L, GATE_RANK, RWKV_WIDTH), GATE_RANK ** -0.5),
        "rwkv_k_k": 1.0 + nrm(ks[19], (L, RWKV_WIDTH), 0.1),
        "rwkv_k_a": 1.0 + nrm(ks[20], (L, RWKV_WIDTH), 0.1),
        "rwkv_r_k": nrm(ks[21], (L, RWKV_HEADS, RWKV_HEAD), 0.1),
        "rwkv_ln_g": 1.0 + nrm(ks[22], (L, RWKV_WIDTH), 0.02),
        "rwkv_ln_b": nrm(ks[23], (L, RWKV_WIDTH), 0.01),
        "pool_w": nrm(ks[24], (L, len(POOL_WINDOWS), POOL_GROUP, POOL_GROUP), POOL_GROUP ** -0.5),
        "pool_scale": 1.0 + nrm(ks[25], (L, POOL_WIDTH), 0.1),
        "conv_w": nrm(ks[26], (L, CONV_TAPS, CONV_WIDTH), CONV_TAPS ** -0.5),
        "w_out": nrm(ks[27], (L, MIX_WIDTH, D_MODEL), MIX_WIDTH ** -0.5),
        "mlp_w1": nrm(ks[28], (L, D_MODEL, FF_HIDDEN), D_MODEL ** -0.5),
        "mlp_w2": nrm(ks[29], (L, FF_HIDDEN, D_MODEL), FF_HIDDEN ** -0.5),
        "final_norm_g": 1.0 + nrm(ks[30], (D_MODEL,), 0.02),
    }


def reference(x, c, ctx, c_ctx, ada_w, ada_b, norm1_g, norm2_g, w_in, mla_q_norm_g, mla_w_uq,
              mla_kv_norm_g, mla_w_ukv, rwkv_mu, rwkv_w0, rwkv_w2, rwkv_a0, rwkv_a2, rwkv_g2,
              rwkv_k_k, rwkv_k_a, rwkv_r_k, rwkv_ln_g, rwkv_ln_b, pool_w, pool_scale, conv_w,
              w_out, mlp_w1, mlp_w2, final_norm_g):
    rope = axial_rope_tables(x.shape[1])
    h, hc = x, ctx
    for l in range(DEPTH):
        p = {
            'ada_w': ada_w[l], 'ada_b': ada_b[l], 'norm1_g': norm1_g[l], 'norm2_g': norm2_g[l],
            'w_in': w_in[l], 'mla_q_norm_g': mla_q_norm_g[l], 'mla_w_uq': mla_w_uq[l],
            'mla_kv_norm_g': mla_kv_norm_g[l], 'mla_w_ukv': mla_w_ukv[l],
            'rwkv_mu': rwkv_mu[l], 'rwkv_w0': rwkv_w0[l], 'rwkv_w2': rwkv_w2[l],
            'rwkv_a0': rwkv_a0[l], 'rwkv_a2': rwkv_a2[l], 'rwkv_g2': rwkv_g2[l],
            'rwkv_k_k': rwkv_k_k[l], 'rwkv_k_a': rwkv_k_a[l], 'rwkv_r_k': rwkv_r_k[l],
            'rwkv_ln_g': rwkv_ln_g[l], 'rwkv_ln_b': rwkv_ln_b[l],
            'pool_w': pool_w[l], 'pool_scale': pool_scale[l], 'conv_w': conv_w[l],
            'w_out': w_out[l], 'mlp_w1': mlp_w1[l], 'mlp_w2': mlp_w2[l],
        }
        h, hc = trunk_layer(h, hc, c, c_ctx, p, rope, l < DEPTH - 1)
    return rmsnorm(h, final_norm_g)
```

```python
import contextlib
import numpy as np
import concourse.bass as bass
import concourse.mybir as mybir
from concourse.bass_utils import run_bass_kernel_spmd

F32 = mybir.dt.float32
BF16 = mybir.dt.bfloat16
AF = mybir.ActivationFunctionType
ALU = mybir.AluOpType
AX = mybir.AxisListType


class Buf:
    __slots__ = ("name", "t", "lw", "rd", "dsem", "dsem_sw", "dcnt")

    def __init__(self, name, t=None):
        self.name = name
        self.t = t
        self.lw = None
        self.rd = []
        self.dsem = None
        self.dsem_sw = None
        self.dcnt = 0

    def __getitem__(self, k):
        return self.t[k]


class DSem:
    __slots__ = ("sem", "cnt")

    def __init__(self, sem):
        self.sem = sem
        self.cnt = 0


class EngS:
    def __init__(self, name, sem):
        self.name = name
        self.sem = sem
        self.cnt = 0
        self.ops = []
        self.seen = {}


class Sched:
    ENGS = ("pe", "act", "dve", "pool", "sp")

    def __init__(self, nc, stack):
        self.nc = nc
        self.stack = stack
        self.E = {}
        for n in self.ENGS:
            sem = stack.enter_context(nc.semaphore("s_" + n))
            self.E[n] = EngS(n, sem)
        self.dma_sems = []
        self.free_ds = {"dsem": [], "dsem_sw": []}
        self.phase_bufs = []
        self.n_inst = 0
        self.uid = 0

    def buf(self, name, t=None):
        return Buf(name, t)

    def sub(self, b, name=""):
        nb = Buf(b.name + name, b.t)
        self.phase_bufs.append(nb)
        return nb

    def sb(self, ctx, name, shape, dt):
        self.uid += 1
        name = f"{name}_{self.uid}"
        t = ctx.enter_context(self.nc.sbuf_tensor(name, list(shape), dt))
        b = Buf(name, t)
        self.phase_bufs.append(b)
        return b

    def ps(self, ctx, name, shape, dt=F32):
        self.uid += 1
        name = f"{name}_{self.uid}"
        t = ctx.enter_context(self.nc.psum_tensor(name, list(shape), dt))
        return Buf(name, t)

    def dram(self, name, shape, dt, kind="Internal"):
        t = self.nc.dram_tensor(name, list(shape), dt, kind=kind)
        return Buf(name, t)

    def _dsem(self, b, sw=False):
        at = "dsem_sw" if sw else "dsem"
        if getattr(b, at) is None:
            if self.free_ds[at]:
                ds = self.free_ds[at].pop()
            else:
                self.uid += 1
                ds = DSem(self.stack.enter_context(self.nc.semaphore(f"d{self.uid}")))
                self.dma_sems.append(ds)
            setattr(b, at, ds)
        return getattr(b, at)

    def recycle(self):
        for b in self.phase_bufs:
            for at in ("dsem", "dsem_sw"):
                if getattr(b, at) is not None:
                    self.free_ds[at].append(getattr(b, at))
                    setattr(b, at, None)
        self.phase_bufs = []

    def _wait(self, e, ev, same=True):
        if ev is None:
            return
        if ev[0] == 'c':
            src = self.E[ev[1]]
            if src is e and not same:
                return
            sem, val = src.sem, ev[2]
        else:
            ds = ev[1]
            sem, val = ds.sem, ds.cnt * 16
        key = id(sem)
        if e.seen.get(key, 0) >= val:
            return
        e.seen[key] = val
        e.ops.append(lambda h, sem=sem, val=val: h.wait_ge(sem, val))

    def _deps(self, e, reads, writes, same=True):
        for r in reads:
            self._wait(e, r.lw, same)
        for w in writes:
            self._wait(e, w.lw, same)
            for ev in w.rd:
                self._wait(e, ev, same)

    def _mark(self, ev, reads, writes):
        for w in writes:
            w.lw = ev
            w.rd = []
        for r in reads:
            if r not in writes:
                r.rd.append(ev)
                if len(r.rd) > 64:
                    last = {}
                    for x in r.rd:
                        last[(x[0], x[1] if x[0] == 'c' else id(x[1]))] = x
                    r.rd = list(last.values())

    def op(self, eng, fn, reads=(), writes=(), same=True):
        e = self.E[eng]
        self._deps(e, reads, writes, same)
        e.cnt += 1
        sem = e.sem
        e.ops.append(lambda h, fn=fn, sem=sem: fn(h).then_inc(sem, 1))
        ev = ('c', eng, e.cnt)
        self._mark(ev, reads, writes)
        self.n_inst += 1
        return ev

    def dma(self, queue, out_ap, in_ap, reads=(), writes=(), owner=None, **kw):
        e = self.E[queue]
        self._deps(e, reads, writes)
        if owner is None:
            owner = writes[0] if writes else reads[0]
        ds = self._dsem(owner, sw=(queue == "pool"))
        ds.cnt += 1
        sem = ds.sem
        e.ops.append(lambda h, o=out_ap, i=in_ap, sem=sem, kw=kw:
                     h.dma_start(out=o, in_=i, **kw).then_inc(sem, 16))
        ev = ('d', ds)
        self._mark(ev, reads, writes)
        self.n_inst += 1
        return ev

    def barrier(self):
        evs = [('c', n, s.cnt) for n, s in self.E.items() if s.cnt > 0]
        devs = [('d', ds) for ds in self.dma_sems if ds.cnt > 0]
        for e in self.E.values():
            for ev in evs:
                if ev[1] != e.name:
                    self._wait(e, ev)
            for ev in devs:
                self._wait(e, ev)

    def flush(self):
        nc = self.nc
        E = self.E
        with nc.Block() as block:
            @block.tensor
            def _(h):
                for f in E["pe"].ops:
                    f(h)

            @block.scalar
            def _(h):
                for f in E["act"].ops:
                    f(h)

            @block.vector
            def _(h):
                for f in E["dve"].ops:
                    f(h)

            @block.gpsimd
            def _(h):
                for f in E["pool"].ops:
                    f(h)

            @block.sync
            def _(h):
                for f in E["sp"].ops:
                    f(h)
        for e in E.values():
            e.ops = []


L = 2
D = 2048
NT = 2304
NCTX = 256
NLAT = 2048
NJ = 16
FF = 8192
NTILE = NT // 128
EPS = 1e-6
GN_EPS = 64e-5
TBS = [(0, 256), (256, 768), (768, 1280), (1280, 1792), (1792, 2304)]

def _cbs():
    cbs = []
    for i in range(3):
        cbs.append(("q", i, [(i * 128, (i + 1) * 128)]))
    cbs.append(("kv", 0, [(384, 512)]))
    cbs.append(("kr", 0, [(512, 576)]))
    cbs.append(("krs", 0, [(528, 544), (512, 528), (560, 576), (544, 560)]))
    for i in range(17):
        cbs.append(("rw", i, [(576 + i * 128, 576 + (i + 1) * 128)]))
    for i in range(4):
        cbs.append(("pool", i, [(2752 + i * 128, 2752 + (i + 1) * 128)]))
    for i in range(4):
        cbs.append(("gb", i, [(3264 + i * 128, 3264 + (i + 1) * 128)]))
        cbs.append(("gc", i, [(3776 + i * 128, 3776 + (i + 1) * 128)]))
        cbs.append(("hx", i, [(4288 + i * 128, 4288 + (i + 1) * 128)]))
    return cbs


CBS = _cbs()
NCB = len(CBS)

PB_N1, PB_N2, PB_QG, PB_KVG, PB_MU, PB_W0, PB_A0, PB_KK, PB_KA, PB_RK, PB_LNG, PB_LNB, PB_PS, PB_CW = \
    0, 16, 32, 35, 36, 53, 61, 69, 73, 77, 81, 85, 89, 93
PB_ROWS = 105


PADL = 8
LAT0 = PADL + NCTX + PADL
WPAD = LAT0 + NLAT + PADL
GROUPS = [list(range(a // 128, b // 128)) for (a, b) in TBS]
SCALE = 192.0 ** -0.5
LDC = -0.6065306597126334


def pcol(t):
    return PADL + t if t < NCTX else t + 2 * PADL


def build(test=None, layers=L, FFt=FF, need_ctx_override=None):
    nc = bass.Bass("TRN2", target_bir_lowering=False)
    NHB = FFt // 128
    NPC = FFt // 512
    T_IN = {None: [], "A": [], "BC": ["modT_d"], "MLA": ["fmla", "modT_d"], "RWKV": ["fB", "modT_d"],
            "D": ["mixedT", "modT_d", "modrow"], "E": ["h1d", "xn2T", "modT_d"]}[test]
    T_OUT = {None: [], "A": ["modrow", "modT_d"], "BC": ["fmla", "fB", "mixedT"], "MLA": ["mixedT"], "RWKV": ["mixedT", "dbgT"],
             "D": ["h1d", "xn2T"], "E": ["hbuf"]}[test]
    ext = {}

    def X(name, shape, dt=F32):
        if name not in ext:
            ext[name] = nc.dram_tensor(name, list(shape), dt, kind="ExternalInput")
        return ext[name]
    nc._ext = ext
    out = nc.dram_tensor("out", [NLAT, D], F32, kind="ExternalOutput")

    with contextlib.ExitStack() as st:
        S = Sched(nc, st)
        IN = S.buf("inputs")
        OUT = S.buf("out")

        def scr(name, shape, dt):
            kind = "ExternalInput" if name in T_IN else ("ExternalOutput" if name in T_OUT else "Internal")
            b = S.dram(name, shape, dt, kind=kind)
            if kind == "ExternalInput":
                ext[name] = b.t
            return b

        LR = range(layers)
        full = test is None
        w_in_bf = [scr(f"w_in_bf{l}", [NCB, 128, NJ, 128], BF16) for l in LR] if (full or test == "BC") else None
        w_out_bf = [scr(f"w_out_bf{l}", [NJ, 128, D], BF16) for l in LR] if (full or test == "D") else None
        w1_bf = [scr(f"w1_bf{l}", [NPC, 128, NJ, 512], BF16) for l in LR] if (full or test == "E") else None
        w2_bf = [scr(f"w2_bf{l}", [16, 128, NHB, 128], BF16) for l in LR] if (full or test == "E") else None
        modrow = scr("modrow", [L, 192, 128], F32)
        modT_d = scr("modT_d", [L, 128, 192], F32)
        fB = scr("fB", [2176, NT], F32)
        fmla = scr("fmla", [640, NT], F32)
        mixedT = scr("mixedT", [D, NT], BF16)
        h1d = scr("h1d", [NT, D], F32)
        xn2T = scr("xn2T", [128, NJ, NT], BF16)
        hbuf = scr("hbuf", [NT, D], F32)
        dbgT = scr("dbgT", [128, 8, NT], F32) if test == "RWKV" else None

        ident = S.sb(st, "ident", [128, 128], F32)
        identb = S.sb(st, "identb", [128, 128], BF16)
        onesb = S.sb(st, "onesb", [128, 128], BF16)
        ones = S.sb(st, "ones", [128, 128], F32)
        blk64 = S.sb(st, "blk64", [128, 128], F32)
        tabG = S.sb(st, "tabG", [128, 48], F32)
        tabA = [S.sb(st, f"tabA{l}", [128, 96], F32) for l in LR]
        tabB = [S.sb(st, f"tabB{l}", [128, PB_ROWS], F32) for l in LR]
        modT = [S.sb(st, f"modT{l}", [128, 96, 2], F32) for l in LR]
        gsc = [S.sb(st, f"gsc{l}", [128, 2, NJ, 2], F32) for l in LR]

        def V(eng, fn, *args, R=(), W=(), **kw):
            S.op(eng, lambda h: getattr(h, fn)(*args, **kw), reads=R, writes=W)

        def mm(o, lhsT, rhs, start, stop, R, W):
            S.op("pe", lambda h: h.matmul(o, lhsT, rhs, start=start, stop=stop), reads=R, writes=W, same=False)

        def tr(o, i, idn, R, W):
            S.op("pe", lambda h: h.transpose(o, i, idn), reads=R, writes=W, same=False)

        rr = [0]

        def evac(o, i, R, W, eng=None):
            if eng is None:
                rr[0] ^= 1
                eng = "act" if rr[0] else "dve"
            if eng == "act":
                V("act", "copy", o, i, R=R, W=W)
            else:
                V(eng, "tensor_copy", o, i, R=R, W=W)

        def phase_end():
            S.barrier()
            S.flush()
            S.recycle()

        V("pool", "memset", ident[:], 1.0, W=[ident])
        V("pool", "affine_select", ident[:], ident[:], pattern=[[-1, 128]], compare_op=ALU.is_equal,
          fill=0.0, base=0, channel_multiplier=1, R=[ident], W=[ident])
        V("pool", "tensor_copy", identb[:], ident[:], R=[ident], W=[identb])
        V("pool", "memset", onesb[:], 1.0, W=[onesb])
        V("pool", "memset", ones[:], 1.0, W=[ones])
        V("pool", "memset", blk64[:], 0.0, W=[blk64])
        V("pool", "memset", blk64[0:64, 0:64], 1.0, W=[blk64])
        V("pool", "memset", blk64[64:128, 64:128], 1.0, W=[blk64])

        def precast(l):
            if w_in_bf is not None:
                wv = X(f"w_in{l}", [D, 4800]).rearrange("(j p) n -> p j n", p=128)
                for ci, (nm, i, segs) in enumerate(CBS):
                    o = 0
                    for (c0, c1) in segs:
                        S.dma("pool", w_in_bf[l][ci, :, :, o:o + (c1 - c0)], wv[:, :, c0:c1], reads=[IN], writes=[w_in_bf[l]])
                        o += c1 - c0
            if w_out_bf is not None:
                wv = X(f"w_out{l}", [D, D]).rearrange("(j p) n -> j p n", p=128)
                for q in range(4):
                    S.dma("pool", w_out_bf[l][q * 4:(q + 1) * 4], wv[q * 4:(q + 1) * 4], reads=[IN], writes=[w_out_bf[l]])
            if w1_bf is not None:
                wv = X(f"w1{l}", [D, FFt]).rearrange("(j p) n -> p j n", p=128)
                for pc in range(NPC):
                    S.dma("pool", w1_bf[l][pc], wv[:, :, pc * 512:(pc + 1) * 512], reads=[IN], writes=[w1_bf[l]])
                wv = X(f"w2{l}", [FFt, D]).rearrange("(hb p) n -> p hb n", p=128)
                for db in range(16):
                    S.dma("pool", w2_bf[l][db], wv[:, :, db * 128:(db + 1) * 128], reads=[IN], writes=[w2_bf[l]])

        for l in LR:
            precast(l)

        with contextlib.ExitStack() as ph:
            prow = S.sb(ph, "prow", [128, 128], F32)
            tps = S.ps(ph, "tps", [128, 512], F32)

            def table(src_ap, nrows, dst):
                S.dma("sp", prow[0:nrows, :], src_ap, reads=[IN], writes=[prow])
                tr(tps[:, 0:nrows], prow[0:nrows, :], ident[0:nrows, 0:nrows], [prow, ident], [tps])
                evac(dst[:, 0:nrows], tps[:, 0:nrows], [tps], [dst])

            table(X("gvec", [48, 128])[:], 48, tabG)
            for l in LR:
                table(X(f"pB{l}", [PB_ROWS, 128])[:], PB_ROWS, tabB[l])
            if "modT_d" in T_IN:
                for l in LR:
                    S.dma("sp", modT[l][:].rearrange("p k s -> p (k s)"), modT_d[l], reads=[modT_d], writes=[modT[l]])
            else:
                siluT = S.sb(ph, "siluT", [128, NJ, 2], F32)
                adaR = [S.sb(ph, f"adaR{i}", [128, NJ, 512], F32) for i in range(2)]
                modps = S.ps(ph, "modps", [128, 512], F32)
                mrow = S.sb(ph, "mrow", [128, 2, 128], F32)
                for l in LR:
                    table(X(f"ada_b{l}", [96, 128])[:], 96, tabA[l])
                V("act", "activation", siluT[:, :, 0], tabG[:, 0:16], AF.Silu, R=[tabG], W=[siluT])
                V("act", "activation", siluT[:, :, 1], tabG[:, 16:32], AF.Silu, R=[tabG], W=[siluT])
                for l in LR:
                    av = X(f"ada_w{l}", [D, 6 * D]).rearrange("(j p) n -> p j n", p=128)
                    for pc in range(24):
                        b = adaR[pc % 2]
                        S.dma("sp" if pc % 2 == 0 else "act", b[:], av[:, :, pc * 512:(pc + 1) * 512], reads=[IN], writes=[b])
                        for q in range(4):
                            blk = pc * 4 + q
                            for j in range(NJ):
                                mm(modps[:, blk * 2:blk * 2 + 2], b[:, j, q * 128:(q + 1) * 128], siluT[:, j, :],
                                   j == 0, j == NJ - 1, [b, siluT], [modps])
                    V("dve", "tensor_tensor", modT[l][:], modps[:, 0:192].rearrange("p (k s) -> p k s", s=2),
                      tabA[l][:].unsqueeze(2).to_broadcast([128, 96, 2]), ALU.add, R=[modps, tabA[l]], W=[modT[l]])
                    mflat = modT[l][:].rearrange("p k s -> p (k s)")
                    S.dma("sp", modT_d[l], mflat, reads=[modT[l]], writes=[modT_d])
                    tr(tps[:, 0:128], mflat[:, 0:128], ident[:], [modT[l], ident], [tps])
                    evac(mrow[:, 0, :], tps[:, 0:128], [tps], [mrow])
                    tr(tps[0:64, 0:128], mflat[:, 128:192], ident[:], [modT[l], ident], [tps])
                    evac(mrow[0:64, 1, :], tps[0:64, 0:128], [tps], [mrow])
                    S.dma("sp", modrow[l, 0:128, :], mrow[:, 0, :], reads=[mrow], writes=[modrow])
                    S.dma("sp", modrow[l, 128:192, :], mrow[0:64, 1, :], reads=[mrow], writes=[modrow])
            for l in LR:
                for s in range(2):
                    V("dve", "scalar_tensor_tensor", gsc[l][:, 0, :, s], modT[l][:, 16:32, s], 1.0, tabB[l][:, PB_N1:PB_N1 + 16],
                      ALU.add, ALU.mult, R=[modT[l], tabB[l]], W=[gsc[l]])
                    V("dve", "scalar_tensor_tensor", gsc[l][:, 1, :, s], modT[l][:, 64:80, s], 1.0, tabB[l][:, PB_N2:PB_N2 + 16],
                      ALU.add, ALU.mult, R=[modT[l], tabB[l]], W=[gsc[l]])
            phase_end()
        if test == "A":
            return nc

        def phase_BC(l):
            tB = tabB[l]
            hsrc = X("hin", [NT, D]) if l == 0 else hbuf.t
            hsrcB = IN if l == 0 else hbuf
            with contextlib.ExitStack() as pc:
                xnT = S.sb(pc, "xnT", [128, NJ, NT], BF16)
                xg = [S.sub(xnT, f"g{i}") for i in range(5)]
                with contextlib.ExitStack() as pb:
                    hts = [S.sb(pb, f"ht{i}", [128, D], F32) for i in range(2)]
                    junk = S.sb(pb, "junk", [128, D], BF16)
                    xs = [S.sb(pb, f"xs{i}", [128, D], BF16) for i in range(4)]
                    ssb = [S.sb(pb, f"ss{i}", [128, 4], F32) for i in range(4)]
                    trp = [S.ps(pb, f"trp{i}", [128, 1024], BF16) for i in range(2)]
                    cnt = 0
                    for gi, tiles in enumerate(GROUPS):
                        sidx = 1 if gi == 0 else 0
                        for k, ti in enumerate(tiles):
                            ht = hts[cnt % 2]
                            cnt += 1
                            S.dma("sp", ht[:], hsrc[ti * 128:(ti + 1) * 128, :], reads=[hsrcB], writes=[ht])
                            sk = ssb[k]
                            V("act", "activation", junk[:], ht[:], AF.Square, accum_out=sk[:, 0:1], R=[ht], W=[junk, sk])
                            V("dve", "tensor_scalar", sk[:, 1:2], sk[:, 0:1], 1.0 / D, EPS, ALU.mult, ALU.add, R=[sk], W=[sk])
                            V("act", "activation", sk[:, 1:2], sk[:, 1:2], AF.Sqrt, R=[sk], W=[sk])
                            V("dve", "reciprocal", sk[:, 2:3], sk[:, 1:2], R=[sk], W=[sk])
                            V("pool", "tensor_scalar", xs[k][:], ht[:], sk[:, 2:3], None, ALU.mult, R=[ht, sk], W=[xs[k]])
                        n = len(tiles) * 128
                        c0 = tiles[0] * 128
                        for j in range(NJ):
                            tp = trp[j % 2]
                            for k in range(len(tiles)):
                                tr(tp[:, k * 128:(k + 1) * 128], xs[k][:, j * 128:(j + 1) * 128], identb[:], [xs[k], identb], [tp])
                            if j % 2 == 0:
                                V("act", "activation", xnT[:, j, c0:c0 + n], tp[:, 0:n], AF.Identity,
                                  bias=modT[l][:, j, sidx:sidx + 1], scale=gsc[l][:, 0, j, sidx:sidx + 1],
                                  R=[tp, modT[l], gsc[l]], W=[xg[gi]])
                            else:
                                V("dve", "tensor_scalar", xnT[:, j, c0:c0 + n], tp[:, 0:n], gsc[l][:, 0, j, sidx:sidx + 1],
                                  modT[l][:, j, sidx:sidx + 1], ALU.mult, ALU.add, R=[tp, modT[l], gsc[l]], W=[xg[gi]])
                    phase_end()
                wring = [S.sb(pc, f"wr{i}", [128, NJ, 128], BF16) for i in range(3)]
                fps = [S.ps(pc, f"fps{i}", [128, 512], F32) for i in range(4)]
                RAW = [S.sb(pc, f"raw{i}", [128, WPAD], F32) for i in range(2)]
                TMP = [S.sb(pc, f"tmp{i}", [128, WPAD], F32) for i in range(2)]
                GB = S.sb(pc, "gb", [128, WPAD], F32)
                GC = S.sb(pc, "gc", [128, WPAD], F32)
                ICT = S.sb(pc, "ict", [128, WPAD], F32)
                ZB = S.sb(pc, "zb", [128, WPAD], BF16)
                OB = [S.sb(pc, f"ob{i}", [128, NT], BF16) for i in range(2)]
                pwt = S.sb(pc, "pwt", [128, 4, 128], BF16)
                invcnt = X("invcnt", [4, NT])
                S.dma("pool", pwt[:], X(f"pool_w{l}", [4, 128, 128]).rearrange("g i o -> i g o"), reads=[IN], writes=[pwt])
                for b in RAW + TMP + [ICT]:
                    V("pool", "memset", b[:], 0.0, W=[b])
                fcnt = [0]
                rawc = [0]
                obc = [0]
                lo, hi = PADL, WPAD - PADL

                def gemm(ci, epi):
                    nm, bi, segs = CBS[ci]
                    M = sum(c1 - c0 for c0, c1 in segs)
                    wt = wring[ci % 3]
                    S.dma("sp" if ci % 2 == 0 else "act", wt[:, :, 0:M], w_in_bf[l][ci, :, :, 0:M], reads=[w_in_bf[l]], writes=[wt])
                    for ti, (t0, t1) in enumerate(TBS):
                        ps = fps[fcnt[0] % 4]
                        fcnt[0] += 1
                        n = t1 - t0
                        for j in range(NJ):
                            mm(ps[0:M, 0:n], wt[:, j, 0:M], xnT[:, j, t0:t1], j == 0, j == NJ - 1, [wt, xg[ti]], [ps])
                        epi(ti, ps, n, M)

                def store_rows(dst, row0, src, M=128):
                    S.dma("sp", dst[row0:row0 + M, 0:NCTX], src[0:M, PADL:PADL + NCTX], reads=[src], writes=[dst])
                    S.dma("sp", dst[row0:row0 + M, NCTX:NT], src[0:M, LAT0:LAT0 + NLAT], reads=[src], writes=[dst])

                def epi_to(dst, eng="act"):
                    def epi(ti, ps, n, M):
                        t0, t1 = TBS[ti]
                        evac(dst[0:M, pcol(t0):pcol(t0) + n], ps[0:M, 0:n], [ps], [dst], eng=eng)
                    return epi

                for ci, (nm, bi, segs) in enumerate(CBS):
                    if nm in ("q", "kv", "kr", "krs"):
                        row0 = {"q": bi * 128, "kv": 384, "kr": 512, "krs": 576}[nm]
                        stg = RAW[rawc[0] % 2]
                        rawc[0] += 1
                        gemm(ci, epi_to(stg, eng=None))
                        store_rows(fmla, row0, stg, M=sum(c1 - c0 for c0, c1 in segs))
                    elif nm == "rw":
                        raw = RAW[rawc[0] % 2]
                        tmp = TMP[rawc[0] % 2]
                        rawc[0] += 1
                        gemm(ci, epi_to(raw))
                        V("pool", "tensor_tensor", tmp[:, lo:hi], raw[:, lo - 1:hi - 1], raw[:, lo + 1:hi + 1], ALU.add, R=[raw], W=[tmp])
                        V("dve", "scalar_tensor_tensor", tmp[:, lo:hi], tmp[:, lo:hi], 0.5, raw[:, lo:hi], ALU.mult, ALU.subtract,
                          R=[raw, tmp], W=[tmp])
                        V("dve", "scalar_tensor_tensor", tmp[:, lo:hi], tmp[:, lo:hi], tB[:, PB_MU + bi:PB_MU + bi + 1], raw[:, lo:hi],
                          ALU.mult, ALU.add, R=[raw, tmp, tB], W=[tmp])
                        store_rows(fB, bi * 128, tmp)
                    elif nm == "pool":
                        g = bi
                        win = (2, 4, 8, 16)[g]
                        U = RAW[rawc[0] % 2]
                        rawc[0] += 1
                        S.dma("act", ICT[:, PADL:PADL + NCTX], invcnt[g:g + 1, 0:NCTX].partition_broadcast(128), reads=[IN], writes=[ICT])
                        S.dma("act", ICT[:, LAT0:LAT0 + NLAT], invcnt[g:g + 1, NCTX:NT].partition_broadcast(128), reads=[IN], writes=[ICT])
                        gemm(ci, epi_to(U))
                        src = U
                        step = 1
                        k = 0
                        while step < win:
                            dst = TMP[k % 2]
                            k += 1
                            V("dve", "tensor_tensor", dst[:, step:WPAD], src[:, step:WPAD], src[:, 0:WPAD - step], ALU.add, R=[src], W=[dst])
                            src = dst
                            step *= 2
                        sh = win // 2 - 1
                        other = TMP[k % 2]
                        V("dve", "tensor_tensor", other[:, lo:hi], src[:, lo + sh:hi + sh], ICT[:, lo:hi], ALU.mult, R=[src, ICT], W=[other])
                        V("dve", "tensor_tensor", ZB[:, lo:hi], other[:, lo:hi], U[:, lo:hi], ALU.subtract, R=[other, U], W=[ZB])
                        ob = OB[obc[0] % 2]
                        obc[0] += 1
                        for ti, (t0, t1) in enumerate(TBS):
                            ps = fps[fcnt[0] % 4]
                            fcnt[0] += 1
                            n = t1 - t0
                            mm(ps[:, 0:n], pwt[:, g, :], ZB[:, pcol(t0):pcol(t0) + n], True, True, [pwt, ZB], [ps])
                            V("act", "activation", ob[:, t0:t1], ps[:, 0:n], AF.Copy, scale=tB[:, PB_PS + g:PB_PS + g + 1], R=[ps, tB], W=[ob])
                        S.dma("sp", mixedT[1024 + g * 128:1024 + (g + 1) * 128, :], ob[:], reads=[ob], writes=[mixedT])
                    elif nm == "gb":
                        gemm(ci, epi_to(GB))
                    elif nm == "gc":
                        gemm(ci, epi_to(GC))
                    elif nm == "hx":
                        U = RAW[rawc[0] % 2]
                        rawc[0] += 1
                        Z = TMP[0]

                        def epi(ti, ps, n, M, U=U):
                            c = pcol(TBS[ti][0])
                            V("dve", "tensor_tensor", U[:, c:c + n], ps[:, 0:n], GC[:, c:c + n], ALU.mult, R=[ps, GC], W=[U])
                        gemm(ci, epi)
                        cw = [tB[:, PB_CW + tap * 4 + bi:PB_CW + tap * 4 + bi + 1] for tap in range(3)]
                        V("dve", "tensor_scalar", Z[:, lo:hi], U[:, lo - 1:hi - 1], cw[0], None, ALU.mult, R=[U, tB], W=[Z])
                        V("dve", "scalar_tensor_tensor", Z[:, lo:hi], U[:, lo:hi], cw[1], Z[:, lo:hi], ALU.mult, ALU.add, R=[U, tB, Z], W=[Z])
                        V("dve", "scalar_tensor_tensor", Z[:, lo:hi], U[:, lo + 1:hi + 1], cw[2], Z[:, lo:hi], ALU.mult, ALU.add, R=[U, tB, Z], W=[Z])
                        V("pool", "tensor_tensor", ZB[:, lo:hi], Z[:, lo:hi], GB[:, lo:hi], ALU.mult, R=[Z, GB], W=[ZB])
                        store_rows(mixedT, 1536 + bi * 128, ZB)
                phase_end()

        def phase_MLA(l, need_ctx):
            tB = tabB[l]
            with contextlib.ExitStack() as pm:
                qn = S.sb(pm, "qn", [128, 4, NT], BF16)
                qp = S.sb(pm, "qp", [64, 4, NT], BF16)
                kn = S.sb(pm, "kn", [128, 4, NT], BF16)
                kp = S.sb(pm, "kp", [64, NT], BF16)
                Vt = S.sb(pm, "Vt", [128, NTILE, 512], BF16)
                rope_cs = X("rope_cs", [2, 64, NLAT])
                with contextlib.ExitStack() as p0:
                    kr = S.sb(p0, "kr", [64, NT], F32)
                    krs = S.sb(p0, "krs", [64, NT], F32)
                    cosT = S.sb(p0, "cosT", [64, NLAT], F32)
                    sinT = S.sb(p0, "sinT", [64, NLAT], F32)
                    S.dma("act", kr[:], fmla[512:576, :], reads=[fmla], writes=[kr])
                    S.dma("act", krs[:], fmla[576:640, :], reads=[fmla], writes=[krs])
                    S.dma("sp", cosT[:], rope_cs[0], reads=[IN], writes=[cosT])
                    S.dma("sp", sinT[:], rope_cs[1], reads=[IN], writes=[sinT])
                    V("dve", "tensor_copy", kp[:, 0:NCTX], kr[:, 0:NCTX], R=[kr], W=[kp])
                    V("dve", "tensor_tensor", kr[:, NCTX:NT], kr[:, NCTX:NT], cosT[:], ALU.mult, R=[kr, cosT], W=[kr])
                    V("pool", "tensor_tensor", krs[:, NCTX:NT], krs[:, NCTX:NT], sinT[:], ALU.mult, R=[krs, sinT], W=[krs])
                    V("dve", "tensor_tensor", kp[:, NCTX:NT], kr[:, NCTX:NT], krs[:, NCTX:NT], ALU.add, R=[kr, krs], W=[kp])
                    phase_end()
                with contextlib.ExitStack() as p1:
                    qlat = S.sb(p1, "qlat", [128, 3, NT], F32)
                    kvlat = S.sb(p1, "kvlat", [128, NT], F32)
                    cosT = S.sb(p1, "cosT", [64, NLAT], F32)
                    sinT = S.sb(p1, "sinT", [64, NLAT], F32)
                    sqb = S.sb(p1, "sqb", [128, 3, 512], F32)
                    rq = S.sb(p1, "rq", [128, NT], F32)
                    rkv = S.sb(p1, "rkv", [128, NT], F32)
                    qg = S.sb(p1, "qg", [128, 3, NT], BF16)
                    kvg = S.sb(p1, "kvg", [128, NT], BF16)
                    wuq = S.sb(p1, "wuq", [128, 3, 768], BF16)
                    wsw = S.sb(p1, "wsw", [128, 3, 4, 64], BF16)
                    wukv = S.sb(p1, "wukv", [128, 1024], BF16)
                    t1 = S.sb(p1, "t1", [64, 512], F32)
                    t2 = S.sb(p1, "t2", [64, 512], F32)
                    ps = [S.ps(p1, f"mps{i}", [128, 512], F32) for i in range(6)]
                    pcnt = [0]

                    def nps():
                        pcnt[0] += 1
                        return ps[pcnt[0] % 6]
                    for c in range(3):
                        S.dma("sp", qlat[:, c, :], fmla[c * 128:(c + 1) * 128, :], reads=[fmla], writes=[qlat])
                    S.dma("act", kvlat[:], fmla[384:512, :], reads=[fmla], writes=[kvlat])
                    S.dma("sp", cosT[:], rope_cs[0], reads=[IN], writes=[cosT])
                    S.dma("sp", sinT[:], rope_cs[1], reads=[IN], writes=[sinT])
                    S.dma("pool", wuq[:], X(f"w_uq{l}", [384, 768]).rearrange("(c p) n -> p c n", p=128), reads=[IN], writes=[wuq])
                    S.dma("pool", wukv[:], X(f"w_ukv{l}", [128, 1024])[:], reads=[IN], writes=[wukv])
                    for h in range(4):
                        b0 = h * 192 + 128
                        for k, (d0, s0) in enumerate([(0, 16), (16, 0), (32, 48), (48, 32)]):
                            V("pool", "tensor_copy", wsw[:, :, h, d0:d0 + 16], wuq[:, :, b0 + s0:b0 + s0 + 16], R=[wuq], W=[wsw])
                    for (t0, t1_) in TBS:
                        n = t1_ - t0
                        V("act", "activation", sqb[:, :, 0:n], qlat[:, :, t0:t1_], AF.Square, R=[qlat], W=[sqb])
                        p = nps()
                        for c in range(3):
                            mm(p[:, 0:n], ones[:], sqb[:, c, 0:n], c == 0, c == 2, [ones, sqb], [p])
                        V("dve", "tensor_scalar", rq[:, t0:t1_], p[:, 0:n], 1.0 / 384, EPS, ALU.mult, ALU.add, R=[p], W=[rq])
                    V("act", "activation", rq[:], rq[:], AF.Sqrt, R=[rq], W=[rq])
                    V("dve", "reciprocal", rq[:], rq[:], R=[rq], W=[rq])
                    for (t0, t1_) in TBS:
                        n = t1_ - t0
                        V("act", "activation", sqb[:, 0, 0:n], kvlat[:, t0:t1_], AF.Square, R=[kvlat], W=[sqb])
                        p = nps()
                        mm(p[:, 0:n], ones[:], sqb[:, 0, 0:n], True, True, [ones, sqb], [p])
                        V("dve", "tensor_scalar", rkv[:, t0:t1_], p[:, 0:n], 1.0 / 128, EPS, ALU.mult, ALU.add, R=[p], W=[rkv])
                    V("act", "activation", rkv[:], rkv[:], AF.Sqrt, R=[rkv], W=[rkv])
                    V("dve", "reciprocal", rkv[:], rkv[:], R=[rkv], W=[rkv])
                    for c in range(3):
                        V("dve", "scalar_tensor_tensor", qg[:, c, :], qlat[:, c, :], tB[:, PB_QG + c:PB_QG + c + 1], rq[:], ALU.mult, ALU.mult,
                          R=[qlat, tB, rq], W=[qg])
                    V("dve", "scalar_tensor_tensor", kvg[:], kvlat[:], tB[:, PB_KVG:PB_KVG + 1], rkv[:], ALU.mult, ALU.mult,
                      R=[kvlat, tB, rkv], W=[kvg])
                    for h in range(4):
                        for ti, (t0, t1_) in enumerate(TBS):
                            if ti == 0 and not need_ctx:
                                qctx = False
                            else:
                                qctx = True
                            n = t1_ - t0
                            if qctx:
                                p = nps()
                                for c in range(3):
                                    mm(p[:, 0:n], wuq[:, c, h * 192:h * 192 + 128], qg[:, c, t0:t1_], c == 0, c == 2, [wuq, qg], [p])
                                V("act", "activation", qn[:, h, t0:t1_], p[:, 0:n], AF.Copy, scale=SCALE, R=[p], W=[qn])
                                pa = nps()
                                for c in range(3):
                                    mm(pa[0:64, 0:n], wuq[:, c, h * 192 + 128:h * 192 + 192], qg[:, c, t0:t1_], c == 0, c == 2, [wuq, qg], [pa])
                                if ti == 0:
                                    V("act", "activation", qp[:, h, t0:t1_], pa[0:64, 0:n], AF.Copy, scale=SCALE, R=[pa], W=[qp])
                                else:
                                    pb_ = nps()
                                    for c in range(3):
                                        mm(pb_[0:64, 0:n], wsw[:, c, h, :], qg[:, c, t0:t1_], c == 0, c == 2, [wsw, qg], [pb_])
                                    l0 = t0 - NCTX
                                    V("dve", "tensor_tensor", t1[:, 0:n], pa[0:64, 0:n], cosT[:, l0:l0 + n], ALU.mult, R=[pa, cosT], W=[t1])
                                    V("dve", "tensor_tensor", t2[:, 0:n], pb_[0:64, 0:n], sinT[:, l0:l0 + n], ALU.mult, R=[pb_, sinT], W=[t2])
                                    V("pool", "tensor_tensor", t1[:, 0:n], t1[:, 0:n], t2[:, 0:n], ALU.add, R=[t1, t2], W=[t1])
                                    V("act", "activation", qp[:, h, t0:t1_], t1[:, 0:n], AF.Copy, scale=SCALE, R=[t1], W=[qp])
                            p = nps()
                            mm(p[:, 0:n], wukv[:, h * 256:h * 256 + 128], kvg[:, t0:t1_], True, True, [wukv, kvg], [p])
                            evac(kn[:, h, t0:t1_], p[:, 0:n], [p], [kn])
                    for i in range(NTILE):
                        p = nps()
                        for h in range(4):
                            mm(p[:, h * 128:(h + 1) * 128], kvg[:, i * 128:(i + 1) * 128], wukv[:, h * 256 + 128:h * 256 + 256], True, True,
                               [kvg, wukv], [p])
                        evac(Vt[:, i, :], p[:], [p], [Vt])
                    phase_end()
                with contextlib.ExitStack() as p2:
                    scps = [S.ps(p2, f"sc{i}", [128, 512], F32) for i in range(3)]
                    ops = [S.ps(p2, f"op{i}", [128, 512], F32) for i in range(2)]
                    sps = [S.ps(p2, f"sp{i}", [128, 512], F32) for i in range(2)]
                    pr = [S.sb(p2, f"pr{i}", [128, 512], BF16) for i in range(3)]
                    rs = [S.sb(p2, f"rs{i}", [128, 512], F32) for i in range(2)]
                    ob = [S.sb(p2, f"ob{i}", [128, 512], BF16) for i in range(2)]
                    it = 0
                    jobs = []
                    for h in range(4):
                        if need_ctx:
                            jobs.append((h, 0, NCTX, 2))
                        for qb in range(4):
                            jobs.append((h, NCTX + qb * 512, NCTX + (qb + 1) * 512, NTILE))
                    kc = 0
                    for (h, q0, q1, nkt) in jobs:
                        n = q1 - q0
                        o_ps = ops[it % 2]
                        s_ps = sps[it % 2]
                        for kt in range(nkt):
                            sc = scps[kc % 3]
                            P = pr[kc % 3]
                            kc += 1
                            ks = slice(kt * 128, (kt + 1) * 128)
                            mm(sc[:, 0:n], kn[:, h, ks], qn[:, h, q0:q1], True, False, [kn, qn], [sc])
                            mm(sc[:, 0:n], kp[:, ks], qp[:, h, q0:q1], False, True, [kp, qp], [sc])
                            V("act", "activation", P[:, 0:n], sc[:, 0:n], AF.Exp, R=[sc], W=[P])
                            mm(o_ps[:, 0:n], Vt[:, kt, h * 128:(h + 1) * 128], P[:, 0:n], kt == 0, kt == nkt - 1, [Vt, P], [o_ps])
                            mm(s_ps[:, 0:n], onesb[:], P[:, 0:n], kt == 0, kt == nkt - 1, [onesb, P], [s_ps])
                        r = rs[it % 2]
                        o = ob[it % 2]
                        V("dve", "reciprocal", r[:, 0:n], s_ps[:, 0:n], R=[s_ps], W=[r])
                        V("dve", "tensor_tensor", o[:, 0:n], o_ps[:, 0:n], r[:, 0:n], ALU.mult, R=[o_ps, r], W=[o])
                        S.dma("sp", mixedT[h * 128:(h + 1) * 128, q0:q1], o[:, 0:n], reads=[o], writes=[mixedT])
                        it += 1
                    phase_end()

        def phase_RWKV(l, need_ctx):
            tB = tabB[l]
            NCH = NT // 64
            with contextlib.ExitStack() as pr:
                F = lambda name: S.sb(pr, name, [128, NT], F32)
                segm = F("segm")
                rT, kT, vT, kkT = F("rT"), F("kT"), F("vT"), F("kkT")
                yacc, bacc = F("yacc"), F("bacc")
                AT, RT, KT, BT = F("AT"), F("RT"), F("KT"), F("BT")
                c1, c2, c3 = F("c1"), F("c2"), F("c3")
                twa = F("twa")
                Vtok = S.sb(pr, "Vtok", [128, NTILE, 128], F32)
                msk = S.sb(pr, "msk", [128, 4, 128], F32)
                w2d = S.sb(pr, "w2d", [96, 128], F32)
                a2d = S.sb(pr, "a2d", [96, 128], F32)
                g2t = S.sb(pr, "g2t", [128, 2, 128], F32)
                tot = S.sb(pr, "tot", [128, NCH], F32)
                GCt = S.sb(pr, "GCt", [128, NCH], F32)
                Tst = S.sb(pr, "Tst", [128, 128], F32)
                ob = S.sb(pr, "obr", [128, NT], BF16)
                Wsb = S.sb(pr, "Wsb", [128, 128], F32)
                Usb = S.sb(pr, "Usb", [128, 128], F32)
                Ysb = S.sb(pr, "Ysb", [128, 128], F32)
                MAT = [{k: S.sb(pr, f"m{k}{s_}", [128, 2, 128], F32) for k in ("LakT", "MrbT", "MrkT", "XT", "Pa", "PTa", "Pb", "PTb")}
                       for s_ in range(2)]
                TOK = []
                for s_ in range(2):
                    dct = {k: S.sb(pr, f"t{k}{s_}", [128, 128], F32) for k in ("B0", "B1", "K0", "K1")}
                    dct.update({k: S.sb(pr, f"t{k}{s_}", [128, 2, 128], F32) for k in ("Ad", "Bd", "Rd")})
                    for b_ in dct.values():
                        V("pool", "memset", b_[:], 0.0, W=[b_])
                    TOK.append(dct)
                V("pool", "memset", Wsb[:], 0.0, W=[Wsb])
                V("pool", "memset", Usb[:], 0.0, W=[Usb])
                gp = [S.ps(pr, f"gp{i}", [128, 512], F32) for i in range(3)]
                cpA = S.ps(pr, "cpA", [128, 512], F32)
                cpY = S.ps(pr, "cpY", [128, 512], F32)
                gcnt = [0]

                def ngp():
                    gcnt[0] += 1
                    return gp[gcnt[0] % 3]

                S.dma("sp", msk[:], X("masks", [4, 128, 128]).rearrange("k p n -> p k n"), reads=[IN], writes=[msk])
                V("pool", "memset", segm[:], 1.0, W=[segm])
                V("pool", "memset", segm[:].rearrange("p (c k) -> p c k", k=64)[:, :, 0:1], 0.0, W=[segm])
                w2x = X(f"rw_w2{l}", [2, 96, 512])
                a2x = X(f"rw_a2{l}", [2, 96, 512])
                g2x = X(f"rw_g2{l}", [256, 512])

                def colmm(dst_fn, lhsT, src, R):
                    for (t0, t1) in TBS:
                        p = ngp()
                        mm(p[:, 0:t1 - t0], lhsT, src[0:lhsT.shape[0], t0:t1], True, True, R, [p])
                        dst_fn(p, t0, t1)

                for q in range(4):
                    S.dma("sp", rT[:], fB[q * 128:(q + 1) * 128, :], reads=[fB], writes=[rT])
                    S.dma("act", kT[:], fB[512 + q * 128:512 + (q + 1) * 128, :], reads=[fB], writes=[kT])
                    S.dma("sp", vT[:], fB[1024 + q * 128:1024 + (q + 1) * 128, :], reads=[fB], writes=[vT])
                    V("dve", "tensor_scalar", kkT[:], kT[:], tB[:, PB_KK + q:PB_KK + q + 1], None, ALU.mult, R=[kT, tB], W=[kkT])
                    V("act", "activation", c1[:], kkT[:], AF.Square, R=[kkT], W=[c1])
                    colmm(lambda p, t0, t1: evac(c2[:, t0:t1], p[:, 0:t1 - t0], [p], [c2]), blk64[:], c1, [blk64, c1])
                    V("act", "activation", c2[:], c2[:], AF.Sqrt, R=[c2], W=[c2])
                    V("dve", "tensor_scalar", c2[:], c2[:], 1e-12, None, ALU.max, R=[c2], W=[c2])
                    V("dve", "reciprocal", c2[:], c2[:], R=[c2], W=[c2])
                    V("dve", "tensor_tensor", kkT[:], kkT[:], c2[:], ALU.mult, R=[kkT, c2], W=[kkT])
                    for i in range(NTILE):
                        p = ngp()
                        tr(p[:, 0:128], vT[:, i * 128:(i + 1) * 128], ident[:], [vT, ident], [p])
                        evac(Vtok[:, i, :], p[:, 0:128], [p], [Vtok])
                    V("pool", "memset", yacc[:], 0.0, W=[yacc])
                    V("pool", "memset", bacc[:], 0.0, W=[bacc])
                    for d in range(2):
                        S.dma("sp", twa[0:96, :], fB[1536 + d * 96:1536 + (d + 1) * 96, :], reads=[fB], writes=[twa])
                        S.dma("act", w2d[:], w2x[d, :, q * 128:(q + 1) * 128], reads=[IN], writes=[w2d])
                        V("act", "activation", twa[0:96, :], twa[0:96, :], AF.Tanh, R=[twa], W=[twa])
                        w0 = tB[:, PB_W0 + d * 4 + q:PB_W0 + d * 4 + q + 1]
                        colmm(lambda p, t0, t1: V("act", "activation", c1[:, t0:t1], p[:, 0:t1 - t0], AF.Sigmoid, bias=w0, R=[p, tB], W=[c1]),
                              w2d[:], twa, [w2d, twa])
                        V("dve", "tensor_scalar", c1[:], c1[:], LDC, None, ALU.mult, R=[c1], W=[c1])
                        V("dve", "tensor_tensor_scan", c2[:], segm[:], c1[:], 0.0, ALU.mult, ALU.add, R=[segm, c1], W=[c2])
                        c2v = c2[:].rearrange("p (c k) -> p c k", k=64)
                        V("dve", "tensor_copy", tot[:], c2v[:, :, 63], R=[c2], W=[tot])
                        V("act", "activation", GCt[:], tot[:], AF.Exp, R=[tot], W=[GCt])
                        if d == 0:
                            V("dve", "tensor_tensor", c3[:], c2[:], c1[:], ALU.subtract, R=[c2, c1], W=[c3])
                        else:
                            V("dve", "tensor_tensor", c3[:].rearrange("p (c k) -> p c k", k=64),
                              tot[:].unsqueeze(2).to_broadcast([128, NCH, 64]), c2v, ALU.subtract, R=[tot, c2], W=[c3])
                            V("dve", "tensor_tensor", c2[:], c3[:], c1[:], ALU.add, R=[c3, c1], W=[c2])
                        V("act", "activation", c3[:], c3[:], AF.Exp, R=[c3], W=[c3])
                        V("dve", "scalar_tensor_tensor", AT[:], kkT[:], -1.0, c3[:], ALU.mult, ALU.mult, R=[kkT, c3], W=[AT])
                        V("act", "activation", c3[:], c2[:], AF.Exp, R=[c2], W=[c3])
                        V("dve", "tensor_tensor", RT[:], rT[:], c3[:], ALU.mult, R=[rT, c3], W=[RT])
                        V("act", "activation", c2[:], c2[:], AF.Exp, scale=-1.0, R=[c2], W=[c2])
                        S.dma("sp", twa[0:96, :], fB[1728 + d * 96:1728 + (d + 1) * 96, :], reads=[fB], writes=[twa])
                        S.dma("act", a2d[:], a2x[d, :, q * 128:(q + 1) * 128], reads=[IN], writes=[a2d])
                        a0 = tB[:, PB_A0 + d * 4 + q:PB_A0 + d * 4 + q + 1]
                        colmm(lambda p, t0, t1: V("act", "activation", c1[:, t0:t1], p[:, 0:t1 - t0], AF.Sigmoid, bias=a0, R=[p, tB], W=[c1]),
                              a2d[:], twa, [a2d, twa])
                        V("dve", "tensor_tensor", BT[:], kkT[:], c1[:], ALU.mult, R=[kkT, c1], W=[BT])
                        V("dve", "tensor_tensor", BT[:], BT[:], c2[:], ALU.mult, R=[BT, c2], W=[BT])
                        V("dve", "tensor_scalar", c3[:], c1[:], -1.0, tB[:, PB_KA + q:PB_KA + q + 1], ALU.add, ALU.mult, R=[c1, tB], W=[c3])
                        V("dve", "scalar_tensor_tensor", c3[:], c3[:], 1.0, kT[:], ALU.add, ALU.mult, R=[c3, kT], W=[c3])
                        V("dve", "tensor_tensor", c1[:], rT[:], c3[:], ALU.mult, R=[rT, c3], W=[c1])
                        V("dve", "scalar_tensor_tensor", bacc[:], c1[:], tB[:, PB_RK + q:PB_RK + q + 1], bacc[:], ALU.mult, ALU.add,
                          R=[c1, tB, bacc], W=[bacc])
                        V("dve", "tensor_tensor", KT[:], c3[:], c2[:], ALU.mult, R=[c3, c2], W=[KT])
                        if dbgT is not None and q == 0 and d == 0:
                            for k_, b_ in enumerate([AT, RT, KT, BT]):
                                S.dma("sp", dbgT[:, 2 + k_, :], b_[:], reads=[b_], writes=[dbgT])
                        mS, mI = (0, 3) if d == 0 else (1, 2)
                        mST = 1 if d == 0 else 0
                        V("pool", "memset", Tst[:], 0.0, W=[Tst])
                        order = list(range(NTILE)) if d == 0 else [1, 0] + list(range(NTILE - 1, 1, -1))

                        def gen_tok(i, slot):
                            cs = slice(i * 128, (i + 1) * 128)
                            TK = TOK[slot]
                            for nm_, src in (("B", BT), ("K", KT)):
                                p = ngp()
                                tr(p[:, 0:128], src[:, cs], ident[:], [src, ident], [p])
                                evac(TK[nm_ + "0"][0:64, :], p[0:64, 0:128], [p], [TK[nm_ + "0"]])
                                evac(TK[nm_ + "1"][64:128, :], p[64:128, 0:128], [p], [TK[nm_ + "1"]])
                            yield
                            for nm_, src in (("Ad", AT), ("Bd", BT), ("Rd", RT)):
                                V("pool", "tensor_copy", TK[nm_][0:64, 0, :], src[0:64, cs], R=[src], W=[TK[nm_]])
                                V("pool", "tensor_copy", TK[nm_][64:128, 1, :], src[64:128, cs], R=[src], W=[TK[nm_]])
                            yield

                        def gen_pre(i, slot):
                            M = MAT[slot]
                            TK = TOK[slot]
                            cs = slice(i * 128, (i + 1) * 128)
                            At, Bt, Kt = AT[:, cs], BT[:, cs], KT[:, cs]
                            fl = lambda t_: t_[:].rearrange("p h c -> p (h c)")
                            v3 = lambda ap_: ap_.rearrange("p (h c) -> p h c", h=2)

                            def prod(dst, lhsT, rhsd, mk, R):
                                p = ngp()
                                mm(p[:, 0:256], lhsT, fl(rhsd), True, True, R + [rhsd], [p])
                                V("dve", "tensor_tensor", dst[:], v3(p[:, 0:256]), msk[:, mk:mk + 1, :].to_broadcast([128, 2, 128]), ALU.mult,
                                  R=[p, msk], W=[dst])
                            prod(M["Pa"], At, TK["Bd"], mS, [AT])
                            yield
                            prod(M["PTa"], Bt, TK["Ad"], mST, [BT])
                            yield
                            prod(M["LakT"], Kt, TK["Ad"], mST, [KT])
                            yield
                            prod(M["MrbT"], Bt, TK["Rd"], mI, [BT])
                            yield
                            prod(M["MrkT"], Kt, TK["Rd"], mI, [KT])
                            yield
                            V("pool", "tensor_tensor", M["XT"][:], M["PTa"][:], ident[:].unsqueeze(1).to_broadcast([128, 2, 128]), ALU.add,
                              R=[M["PTa"], ident], W=[M["XT"]])
                            P, PT, P2, PT2 = M["Pa"], M["PTa"], M["Pb"], M["PTb"]
                            for lev in range(5):
                                p = ngp()
                                for h in range(2):
                                    mm(p[:, h * 128:(h + 1) * 128], PT[:, h, :], P[:, h, :], True, True, [PT, P], [p])
                                evac(P2[:], v3(p[:, 0:256]), [p], [P2])
                                if lev < 4:
                                    p = ngp()
                                    for h in range(2):
                                        mm(p[:, h * 128:(h + 1) * 128], P[:, h, :], PT[:, h, :], True, True, [PT, P], [p])
                                    evac(PT2[:], v3(p[:, 0:256]), [p], [PT2])
                                yield
                                p = ngp()
                                for h in range(2):
                                    mm(p[:, h * 128:(h + 1) * 128], P2[:, h, :], M["XT"][:, h, :], True, True, [P2, M["XT"]], [p])
                                V("dve", "tensor_tensor", M["XT"][:], v3(p[:, 0:256]), M["XT"][:], ALU.add, R=[p, M["XT"]], W=[M["XT"]])
                                P, PT, P2, PT2 = P2, PT2, P, PT
                                yield

                        def gen_chain(i, slot):
                            M = MAT[slot]
                            TK = TOK[slot]
                            cs = slice(i * 128, (i + 1) * 128)
                            halves = (0, 1) if d == 0 else (1, 0)
                            for hh in halves:
                                tb = hh * 64
                                ch = i * 2 + hh
                                Wp, Up, Tp = cpA[:, 0:128], cpA[:, 128:256], cpA[:, 256:384]
                                mm(Wp, AT[:, cs], Tst[:], True, False, [AT, Tst], [cpA])
                                for h in range(2):
                                    mm(cpA[:, h * 64:(h + 1) * 64], M["LakT"][:, h, :], Vtok[:, i, h * 64:(h + 1) * 64], False, h == 1,
                                       [M["LakT"], Vtok], [cpA])
                                evac(Wsb[:], Wp, [cpA], [Wsb], eng="act")
                                yield
                                for h in range(2):
                                    mm(cpA[:, 128 + h * 64:128 + (h + 1) * 64], M["XT"][:, h, :], Wsb[:, h * 64:(h + 1) * 64], True, True,
                                       [M["XT"], Wsb], [cpA])
                                evac(Usb[:], Up, [cpA], [Usb], eng="dve")
                                yield
                                mm(cpY[:, 0:128], RT[:, cs], Tst[:], True, False, [RT, Tst], [cpY])
                                for h in range(2):
                                    hs = slice(h * 64, (h + 1) * 64)
                                    mm(cpY[:, hs], M["MrbT"][:, h, :], Usb[:, hs], False, False, [M["MrbT"], Usb], [cpY])
                                    mm(cpY[:, hs], M["MrkT"][:, h, :], Vtok[:, i, hs], False, h == 1, [M["MrkT"], Vtok], [cpY])
                                evac(Ysb[tb:tb + 64, :], cpY[tb:tb + 64, 0:128], [cpY], [Ysb], eng="act")
                                mm(Tp, TK["B%d" % hh][:], Usb[:], True, False, [TK["B%d" % hh], Usb], [cpA])
                                mm(Tp, TK["K%d" % hh][:], Vtok[:, i, :], False, False, [TK["K%d" % hh], Vtok], [cpA])
                                mm(Tp, ident[:], Tst[:], False, True, [ident, Tst], [cpA])
                                V("dve", "scalar_tensor_tensor", Tst[:], Tp, GCt[:, ch:ch + 1], blk64[:], ALU.mult, ALU.mult,
                                  R=[cpA, GCt, blk64], W=[Tst])
                                yield

                        def drive(gens):
                            gens = [g for g in gens if g is not None]
                            while gens:
                                for g in list(gens):
                                    try:
                                        next(g)
                                    except StopIteration:
                                        gens.remove(g)

                        def seq(*gs):
                            for g in gs:
                                yield from g
                        drive([seq(gen_tok(order[0], 0), gen_pre(order[0], 0))])
                        for k_, i in enumerate(order):
                            slot = k_ % 2
                            nxt = order[k_ + 1] if k_ + 1 < len(order) else None
                            gens = [gen_chain(i, slot)]
                            if nxt is not None:
                                gens += [seq(gen_tok(nxt, 1 - slot), gen_pre(nxt, 1 - slot))]
                            drive(gens)
                            cs = slice(i * 128, (i + 1) * 128)
                            p = ngp()
                            tr(p[:, 0:128], Ysb[:], ident[:], [Ysb, ident], [p])
                            V("dve", "tensor_tensor", yacc[:, cs], p[:, 0:128], yacc[:, cs], ALU.add, R=[p, yacc], W=[yacc])
                    if dbgT is not None and q == 0:
                        S.dma("sp", dbgT[:, 0, :], yacc[:], reads=[yacc], writes=[dbgT])
                        S.dma("sp", dbgT[:, 1, :], bacc[:], reads=[bacc], writes=[dbgT])
                    colmm(lambda p, t0, t1: V("dve", "tensor_tensor", c1[:, t0:t1], p[:, 0:t1 - t0], vT[:, t0:t1], ALU.mult, R=[p, vT], W=[c1]),
                          blk64[:], bacc, [blk64, bacc])
                    colmm(lambda p, t0, t1: V("dve", "scalar_tensor_tensor", c2[:, t0:t1], p[:, 0:t1 - t0], -1.0 / 64, yacc[:, t0:t1],
                                              ALU.mult, ALU.add, R=[p, yacc], W=[c2]), blk64[:], yacc, [blk64, yacc])
                    V("act", "activation", c3[:], c2[:], AF.Square, R=[c2], W=[c3])
                    colmm(lambda p, t0, t1: V("dve", "tensor_scalar", AT[:, t0:t1], p[:, 0:t1 - t0], 1.0 / 64, GN_EPS, ALU.mult, ALU.add,
                                              R=[p], W=[AT]), blk64[:], c3, [blk64, c3])
                    V("act", "activation", AT[:], AT[:], AF.Sqrt, R=[AT], W=[AT])
                    V("dve", "reciprocal", AT[:], AT[:], R=[AT], W=[AT])
                    V("dve", "tensor_tensor", c2[:], c2[:], AT[:], ALU.mult, R=[c2, AT], W=[c2])
                    V("dve", "tensor_scalar", c2[:], c2[:], tB[:, PB_LNG + q:PB_LNG + q + 1], tB[:, PB_LNB + q:PB_LNB + q + 1], ALU.mult, ALU.add,
                      R=[c2, tB], W=[c2])
                    V("dve", "tensor_tensor", c2[:], c2[:], c1[:], ALU.add, R=[c2, c1], W=[c2])
                    S.dma("sp", RT[:], fB[1920:2048, :], reads=[fB], writes=[RT])
                    S.dma("act", KT[:], fB[2048:2176, :], reads=[fB], writes=[KT])
                    S.dma("sp", g2t[:], g2x.rearrange("(c p) n -> p c n", p=128)[:, :, q * 128:(q + 1) * 128], reads=[IN], writes=[g2t])
                    V("act", "activation", RT[:], RT[:], AF.Sigmoid, R=[RT], W=[RT])
                    V("act", "activation", KT[:], KT[:], AF.Sigmoid, R=[KT], W=[KT])
                    for (t0, t1) in TBS:
                        p = ngp()
                        mm(p[:, 0:t1 - t0], g2t[:, 0, :], RT[:, t0:t1], True, False, [g2t, RT], [p])
                        mm(p[:, 0:t1 - t0], g2t[:, 1, :], KT[:, t0:t1], False, True, [g2t, KT], [p])
                        V("dve", "tensor_tensor", ob[:, t0:t1], p[:, 0:t1 - t0], c2[:, t0:t1], ALU.mult, R=[p, c2], W=[ob])
                    S.dma("sp", mixedT[512 + q * 128:512 + (q + 1) * 128, :], ob[:], reads=[ob], writes=[mixedT])
                phase_end()

        def gate_rows(l, k):
            mv = modrow.t[l].rearrange("(k s) p -> s k p", s=2)
            return [mv[s, k * 16:(k + 1) * 16, :] for s in range(2)]

        def phase_D(l, need_ctx):
            tB = tabB[l]
            hsrc = X("hin", [NT, D]) if l == 0 else hbuf.t
            hsrcB = IN if l == 0 else hbuf
            with contextlib.ExitStack() as pd:
                wout = S.sb(pd, "wout", [128, NJ, D], BF16)
                gtb = [S.sb(pd, f"gtb{s}", [128, NJ, 128], F32) for s in range(2)]
                mts = [S.sb(pd, f"mt{i}", [128, NJ, 128], BF16) for i in range(3)]
                hts = [S.sb(pd, f"hD{i}", [128, D], F32) for i in range(2)]
                h1s = [S.sb(pd, f"h1s{i}", [128, D], F32) for i in range(2)]
                junk = S.sb(pd, "junkD", [128, D], BF16)
                xs = [S.sb(pd, f"xsD{i}", [128, D], BF16) for i in range(4)]
                ssb = [S.sb(pd, f"ssD{i}", [128, 4], F32) for i in range(4)]
                xblk = [S.sb(pd, f"xblk{i}", [128, NJ, 512], BF16) for i in range(2)]
                wops = S.ps(pd, "wops", [128, D], F32)
                trp = [S.ps(pd, f"trpD{i}", [128, 1024], BF16) for i in range(2)]
                for j in range(NJ):
                    S.dma("sp" if j % 2 == 0 else "act", wout[:, j, :], w_out_bf[l][j], reads=[w_out_bf[l]], writes=[wout])
                gr = gate_rows(l, 2)
                for s in range(2):
                    S.dma("sp", gtb[s][:], gr[s].partition_broadcast(128), reads=[modrow], writes=[gtb[s]])
                mv = mixedT.t.rearrange("(j p) t -> p j t", p=128)
                cnt = 0
                for gi, tiles in enumerate(GROUPS):
                    if gi == 0 and not need_ctx:
                        continue
                    sidx = 1 if gi == 0 else 0
                    xb = xblk[gi % 2]
                    for k, ti in enumerate(tiles):
                        mt = mts[cnt % 3]
                        ht = hts[cnt % 2]
                        h1 = h1s[cnt % 2]
                        cnt += 1
                        S.dma("act", mt[:], mv[:, :, ti * 128:(ti + 1) * 128], reads=[mixedT], writes=[mt])
                        S.dma("sp", ht[:], hsrc[ti * 128:(ti + 1) * 128, :], reads=[hsrcB], writes=[ht])
                        for j in range(NJ):
                            for nb in range(4):
                                mm(wops[:, nb * 512:(nb + 1) * 512], mt[:, j, :], wout[:, j, nb * 512:(nb + 1) * 512], j == 0, j == NJ - 1,
                                   [mt, wout], [wops])
                        V("dve", "tensor_tensor", h1[:], wops[:], gtb[sidx][:].rearrange("p j c -> p (j c)"), ALU.mult, R=[wops, gtb[sidx]], W=[h1])
                        V("pool", "tensor_tensor", h1[:], h1[:], ht[:], ALU.add, R=[h1, ht], W=[h1])
                        S.dma("sp", h1d[ti * 128:(ti + 1) * 128, :], h1[:], reads=[h1], writes=[h1d])
                        sk = ssb[k]
                        V("act", "activation", junk[:], h1[:], AF.Square, accum_out=sk[:, 0:1], R=[h1], W=[junk, sk])
                        V("dve", "tensor_scalar", sk[:, 1:2], sk[:, 0:1], 1.0 / D, EPS, ALU.mult, ALU.add, R=[sk], W=[sk])
                        V("act", "activation", sk[:, 1:2], sk[:, 1:2], AF.Sqrt, R=[sk], W=[sk])
                        V("dve", "reciprocal", sk[:, 2:3], sk[:, 1:2], R=[sk], W=[sk])
                        V("pool", "tensor_scalar", xs[k][:], h1[:], sk[:, 2:3], None, ALU.mult, R=[h1, sk], W=[xs[k]])
                    n = len(tiles) * 128
                    c0 = tiles[0] * 128
                    for j in range(NJ):
                        tp = trp[j % 2]
                        for k in range(len(tiles)):
                            tr(tp[:, k * 128:(k + 1) * 128], xs[k][:, j * 128:(j + 1) * 128], identb[:], [xs[k], identb], [tp])
                        if j % 2 == 0:
                            V("act", "activation", xb[:, j, 0:n], tp[:, 0:n], AF.Identity,
                              bias=modT[l][:, 48 + j, sidx:sidx + 1], scale=gsc[l][:, 1, j, sidx:sidx + 1], R=[tp, modT[l], gsc[l]], W=[xb])
                        else:
                            V("dve", "tensor_scalar", xb[:, j, 0:n], tp[:, 0:n], gsc[l][:, 1, j, sidx:sidx + 1],
                              modT[l][:, 48 + j, sidx:sidx + 1], ALU.mult, ALU.add, R=[tp, modT[l], gsc[l]], W=[xb])
                    S.dma("sp", xn2T[:, :, c0:c0 + n], xb[:, :, 0:n], reads=[xb], writes=[xn2T])
                phase_end()

        def phase_E(l, need_ctx, final):
            with contextlib.ExitStack() as pe:
                xb = S.sb(pe, "xbE", [128, NJ, 512], BF16)
                hid = S.sb(pe, "hid", [128, NHB, 512], BF16)
                w1r = [S.sb(pe, f"w1r{i}", [128, NJ, 512], BF16) for i in range(2)]
                w2r = [S.sb(pe, f"w2r{i}", [128, NHB, 128], BF16) for i in range(2)]
                hblk = S.sb(pe, "hblk", [128, 4, D], F32)
                rl = [S.sb(pe, f"rl{i}", [128, 512], F32) for i in range(2)]
                osb = [S.sb(pe, f"osb{i}", [128, 512], F32) for i in range(2)]
                hps = [S.ps(pe, f"hps{i}", [128, 512], F32) for i in range(2)]
                ops_ = [S.ps(pe, f"opsE{i}", [128, 512], F32) for i in range(2)]
                tps = [S.ps(pe, f"tpsE{i}", [128, 512], F32) for i in range(2)]
                if final:
                    gfin = S.sb(pe, "gfin", [128, NJ, 128], F32)
                    ssb = S.sb(pe, "ssE", [128, 4], F32)
                    junk = S.sb(pe, "junkE", [128, D], BF16)
                    S.dma("sp", gfin[:], X("gvec", [48, 128])[32:48, :].partition_broadcast(128), reads=[IN], writes=[gfin])
                wc = [0, 0]
                for ti, (t0, t1) in enumerate(TBS):
                    if ti == 0 and not need_ctx:
                        continue
                    sidx = 1 if ti == 0 else 0
                    n = t1 - t0
                    nt = n // 128
                    S.dma("sp", xb[:, :, 0:n], xn2T[:, :, t0:t1], reads=[xn2T], writes=[xb])
                    S.dma("act", hblk[:, 0:nt, :], h1d.t[t0:t1, :].rearrange("(i p) d -> p i d", p=128), reads=[h1d], writes=[hblk])
                    for pc in range(NPC):
                        w = w1r[wc[0] % 2]
                        wc[0] += 1
                        S.dma("sp" if pc % 2 == 0 else "act", w[:], w1_bf[l][pc], reads=[w1_bf[l]], writes=[w])
                        for q in range(4):
                            hb = pc * 4 + q
                            ps = hps[hb % 2]
                            for j in range(NJ):
                                mm(ps[:, 0:n], w[:, j, q * 128:(q + 1) * 128], xb[:, j, 0:n], j == 0, j == NJ - 1, [w, xb], [ps])
                            r = rl[hb % 2]
                            V("act", "activation", r[:, 0:n], ps[:, 0:n], AF.Relu, R=[ps], W=[r])
                            V("dve" if hb % 2 == 0 else "pool", "tensor_tensor", hid[:, hb, 0:n], r[:, 0:n], r[:, 0:n], ALU.mult, R=[r], W=[hid])
                    for db in range(16):
                        w = w2r[wc[1] % 2]
                        wc[1] += 1
                        S.dma("sp" if db % 2 == 0 else "act", w[:], w2_bf[l][db], reads=[w2_bf[l]], writes=[w])
                        ps = ops_[db % 2]
                        for hb in range(NHB):
                            mm(ps[:, 0:n], w[:, hb, :], hid[:, hb, 0:n], hb == 0, hb == NHB - 1, [w, hid], [ps])
                        o = osb[db % 2]
                        V("act", "activation", o[:, 0:n], ps[:, 0:n], AF.Copy, scale=modT[l][:, 80 + db, sidx:sidx + 1], R=[ps, modT[l]], W=[o])
                        tp = tps[db % 2]
                        for i in range(nt):
                            tr(tp[:, i * 128:(i + 1) * 128], o[:, i * 128:(i + 1) * 128], ident[:], [o, ident], [tp])
                        hv = hblk[:, 0:nt, db * 128:(db + 1) * 128]
                        V("dve", "tensor_tensor", hv, tp[:, 0:n].rearrange("p (i c) -> p i c", c=128), hv, ALU.add, R=[tp, hblk], W=[hblk])
                    if not final:
                        S.dma("sp", hbuf.t[t0:t1, :].rearrange("(i p) d -> p i d", p=128), hblk[:, 0:nt, :], reads=[hblk], writes=[hbuf])
                    else:
                        for i in range(nt):
                            V("act", "activation", junk[:], hblk[:, i, :], AF.Square, accum_out=ssb[:, 0:1], R=[hblk], W=[junk, ssb])
                            V("dve", "tensor_scalar", ssb[:, 1:2], ssb[:, 0:1], 1.0 / D, EPS, ALU.mult, ALU.add, R=[ssb], W=[ssb])
                            V("act", "activation", ssb[:, 1:2], ssb[:, 1:2], AF.Sqrt, R=[ssb], W=[ssb])
                            V("dve", "reciprocal", ssb[:, 2:3], ssb[:, 1:2], R=[ssb], W=[ssb])
                            V("dve", "scalar_tensor_tensor", hblk[:, i, :], hblk[:, i, :], ssb[:, 2:3], gfin[:].rearrange("p j c -> p (j c)"),
                              ALU.mult, ALU.mult, R=[hblk, ssb, gfin], W=[hblk])
                        r0 = t0 - NCTX
                        S.dma("sp", out[r0:r0 + n, :].rearrange("(i p) d -> p i d", p=128), hblk[:, 0:nt, :], reads=[hblk], writes=[OUT])
                phase_end()

        if test is None:
            for l in LR:
                nctx = (l < L - 1)
                phase_BC(l)
                phase_MLA(l, nctx)
                phase_RWKV(l, nctx)
                phase_D(l, nctx)
                phase_E(l, nctx, final=(l == L - 1))
        elif test == "BC":
            phase_BC(0)
        elif test == "MLA":
            phase_MLA(0, True)
        elif test == "RWKV":
            phase_RWKV(0, True)
        elif test == "D":
            phase_D(0, True)
        elif test == "E":
            phase_E(0, True, final=False)
    return nc


def _consts():
    GRID_W = 64
    t = np.arange(NLAT)
    pos = np.stack([t // GRID_W, t % GRID_W], 0).astype(np.float32)
    inv_freq = (10000.0 ** (-np.arange(0, 32, 2, dtype=np.float32) / 32)).astype(np.float32)
    cs = np.zeros((2, 64, NLAT), np.float32)
    for axis in range(2):
        ang = pos[axis][None, :] * inv_freq[:, None]
        for half in range(2):
            f0 = axis * 32 + half * 16
            cs[0, f0:f0 + 16] = np.cos(ang)
            cs[1, f0:f0 + 16] = np.sin(ang) * (-1.0 if half == 0 else 1.0)
    r = np.arange(128)
    same = (r[:, None] // 64) == (r[None, :] // 64)
    m = np.stack([same & (r[None, :] < r[:, None]), same & (r[None, :] > r[:, None]),
                  same & (r[None, :] <= r[:, None]), same & (r[None, :] >= r[:, None])], 0).astype(np.float32)
    ic = np.zeros((4, NT), np.float32)
    for g, w in enumerate((2, 4, 8, 16)):
        for (o, n) in ((0, NCTX), (NCTX, NLAT)):
            tt = np.arange(n)
            lo = np.clip(tt - w // 2, 0, n)
            hi = np.clip(tt + w // 2, 0, n)
            ic[g, o:o + n] = 1.0 / (hi - lo)
    return cs, m, ic


def host_arrays(inp, b, layers=L):
    f = lambda a: np.ascontiguousarray(np.asarray(a, dtype=np.float32))
    cs, m, ic = _consts()
    d = {"rope_cs": cs, "masks": m, "invcnt": ic}
    for l in range(layers):
        rows = [inp["norm1_g"][l], inp["norm2_g"][l], inp["mla_q_norm_g"][l], inp["mla_kv_norm_g"][l],
                inp["rwkv_mu"][l], inp["rwkv_w0"][l].reshape(-1), inp["rwkv_a0"][l].reshape(-1),
                inp["rwkv_k_k"][l], inp["rwkv_k_a"][l], inp["rwkv_r_k"][l].reshape(-1),
                inp["rwkv_ln_g"][l], inp["rwkv_ln_b"][l], inp["pool_scale"][l], inp["conv_w"][l].reshape(-1)]
        pBl = np.concatenate([np.asarray(r, np.float32).reshape(-1, 128) for r in rows], 0)
        assert pBl.shape == (PB_ROWS, 128)
        d.update({
            f"pB{l}": f(pBl), f"ada_b{l}": f(np.asarray(inp["ada_b"][l]).reshape(96, 128)), f"ada_w{l}": inp["ada_w"][l],
            f"w_in{l}": inp["w_in"][l], f"w_uq{l}": inp["mla_w_uq"][l], f"w_ukv{l}": inp["mla_w_ukv"][l],
            f"rw_w2{l}": inp["rwkv_w2"][l], f"rw_a2{l}": inp["rwkv_a2"][l], f"rw_g2{l}": inp["rwkv_g2"][l],
            f"pool_w{l}": inp["pool_w"][l], f"w_out{l}": inp["w_out"][l], f"w1{l}": inp["mlp_w1"][l],
            f"w2{l}": inp["mlp_w2"][l]})
    d["hin"] = np.concatenate([inp["ctx"][b], inp["x"][b]], 0)
    d["gvec"] = np.concatenate([np.asarray(inp["c"][b]).reshape(16, 128), np.asarray(inp["c_ctx"]).reshape(16, 128),
                                np.asarray(inp["final_norm_g"]).reshape(16, 128)], 0)
    return d


def make_in_maps(nc, inp, cores=8, layers=L):
    f = lambda a: np.ascontiguousarray(np.asarray(a, dtype=np.float32))
    maps = []
    cache = {}
    for b in range(cores):
        d = host_arrays(inp, b, layers)
        m = {}
        for k in nc._ext:
            a = d[k]
            if k not in ("hin", "gvec"):
                if k not in cache:
                    cache[k] = f(a)
                a = cache[k]
            else:
                a = f(a)
            m[k] = a
        maps.append(m)
    return maps


_NC = {}


def kernel(**inp):
    if "nc" not in _NC:
        _NC["nc"] = build()
    nc = _NC["nc"]
    maps = make_in_maps(nc, inp)
    res = run_bass_kernel_spmd(nc, maps, core_ids=list(range(8)))
    return np.stack([np.asarray(r["out"]) for r in res.results], 0).astype(np.float32)
```
